# Optimizing a Trainium2 kernel written in Bass

```python
import math
import functools
import jax
import jax.numpy as jnp
from jax import lax
import numpy as np

D_MODEL = 4096
BATCH = 8
SEQ = 2048
DEPTH = 4
DEC_BATCH = 2
DEC_SEQ = 8192
PAST_LEN = 128

D_FF = 4 * D_MODEL
CHUNK = 64
EPS = 1e-6
GROUP_WIDTH = D_MODEL // 2

H_A = 16
DK_A = GROUP_WIDTH // H_A
DV_A = GROUP_WIDTH // H_A
CONV_W = 5

DK_B = 128
H_B = GROUP_WIDTH // DK_B
DV_B = GROUP_WIDTH // H_B

H_C = 4
DK_C = GROUP_WIDTH // (2 * H_C)
DV_C = GROUP_WIDTH // H_C

H_D = 8
DK_D = GROUP_WIDTH // H_D
DV_D = GROUP_WIDTH // H_D
ROPE_BASE = 10000.0
RET_DECAY_MIN_EXP = 5.0

N_EVEN = (DEPTH + 1) // 2
N_ODD = DEPTH // 2

EVEN_SIZES = (H_A * DK_A, H_A * DK_A, H_A * DV_A, H_A * DV_A, 4 * H_A,
              H_B * DK_B, 2 * H_B * DK_B, H_B * DV_B, H_B * DV_B)
ODD_SIZES = (H_C * DK_C, H_C * DK_C, H_C * DV_C, H_C * DV_C, 4 * H_C,
             H_D * DK_D, H_D * DK_D, H_D * DV_D, H_D * DV_D)
EVEN_COLS = sum(EVEN_SIZES)
ODD_COLS = sum(ODD_SIZES)
EVEN_MIX = H_A * DV_A + H_B * DV_B
ODD_MIX = H_C * DV_C + H_D * DV_D
CONV_CH = 2 * H_A * DK_A + H_A * DV_A

kernel_name = 'hybrid_bidir_deltanet_hgrn2_mlstm_retention_encoder'


def rmsnorm(x, w):
    xf = x.astype(jnp.float32)
    y = xf * lax.rsqrt(jnp.mean(xf * xf, axis=-1, keepdims=True) + EPS)
    return (y * w.astype(jnp.float32)).astype(x.dtype)


def head_rmsnorm(x):
    return x * lax.rsqrt(jnp.mean(x * x, axis=-1, keepdims=True) + EPS)


def l2norm(x):
    return x * lax.rsqrt(jnp.sum(x * x, axis=-1, keepdims=True) + EPS)


def split_cols(x, sizes):
    idx, acc = [], 0
    for s in sizes[:-1]:
        acc += s
        idx.append(acc)
    return jnp.split(x, idx, axis=-1)


def flip(t):
    return jnp.flip(t, axis=1)


def reverse_direction(scan_fn, *args):
    return flip(scan_fn(*[flip(a) for a in args]))


def to_chunks(t):
    b, l = t.shape[:2]
    t = t.reshape((b, l // CHUNK, CHUNK) + t.shape[2:])
    return jnp.moveaxis(t, (1, 3), (0, 2))


def from_chunks(t):
    n, b, h, c = t.shape[:4]
    t = jnp.moveaxis(t, (0, 2), (1, 3))
    return t.reshape((b, n * c, h) + t.shape[4:])


def centred_conv(x, w):
    pad = CONV_W // 2
    return lax.conv_general_dilated(
        x, w[:, None, :].astype(x.dtype), window_strides=(1,), padding=[(pad, pad)],
        dimension_numbers=('NWC', 'WIO', 'NWC'), feature_group_count=x.shape[-1])


def rotary(x):
    l, d = x.shape[1], x.shape[-1]
    half = d // 2
    inv = ROPE_BASE ** (-jnp.arange(half, dtype=jnp.float32) / half)
    ang = jnp.arange(l, dtype=jnp.float32)[:, None] * inv[None, :]
    cos = jnp.cos(ang)[None, :, None, :]
    sin = jnp.sin(ang)[None, :, None, :]
    x1, x2 = x[..., :half], x[..., half:]
    return jnp.concatenate([x1 * cos - x2 * sin, x1 * sin + x2 * cos], axis=-1)


def gated_delta_scan(q, k, v, beta, g):
    bsz = q.shape[0]
    q, k, v, beta, g = (to_chunks(t.astype(jnp.float32)) for t in (q, k, v, beta, g))
    incl = jnp.tril(jnp.ones((CHUNK, CHUNK), dtype=bool))
    strict = jnp.tril(jnp.ones((CHUNK, CHUNK), dtype=bool), -1)
    b = jnp.cumsum(g, axis=-1)
    decay = jnp.exp(jnp.where(incl, b[..., :, None] - b[..., None, :], -jnp.inf))
    k_beta = k * beta[..., None]
    lower = jnp.where(strict, jnp.einsum('nbhid,nbhjd->nbhij', k_beta, k) * decay, 0.0)
    t_mat = lower + jnp.eye(CHUNK, dtype=jnp.float32)
    solve = functools.partial(lax.linalg.triangular_solve, left_side=True, lower=True, unit_diagonal=True)
    u = solve(t_mat, v * beta[..., None])
    w = solve(t_mat, k_beta * jnp.exp(b)[..., None])
    attn = jnp.einsum('nbhid,nbhjd->nbhij', q, k) * decay
    b_last = b[..., -1:]
    q_dec = q * jnp.exp(b)[..., None]
    k_tail = k * jnp.exp(b_last - b)[..., None]
    chunk_decay = jnp.exp(b_last)[..., None]

    def step(state, inp):
        q_c, k_c, u_c, w_c, dec_c, attn_c = inp
        v_new = u_c - jnp.einsum('bhcd,bhde->bhce', w_c, state)
        o = jnp.einsum('bhcd,bhde->bhce', q_c, state) + jnp.einsum('bhij,bhje->bhie', attn_c, v_new)
        state = state * dec_c + jnp.einsum('bhcd,bhce->bhde', k_c, v_new)
        return state, o

    s0 = jnp.zeros((bsz, q.shape[2], q.shape[-1], v.shape[-1]), jnp.float32)
    _, o = lax.scan(step, s0, (q_dec, k_tail, u, w, chunk_decay, attn))
    return from_chunks(o)


def gla_scan(q, k, v, log_f):
    bsz = q.shape[0]
    q, k, v, log_f = (to_chunks(t.astype(jnp.float32)) for t in (q, k, v, log_f))
    incl = jnp.tril(jnp.ones((CHUNK, CHUNK), dtype=bool))
    b = jnp.cumsum(log_f, axis=3)
    b_last = b[:, :, :, -1:, :]
    q_dec = q * jnp.exp(b)
    k_tail = k * jnp.exp(b_last - b)
    chunk_decay = jnp.exp(b_last)[:, :, :, 0, :, None]

    def step(state, inp):
        q_c, k_c, v_c, b_c, qd_c, kt_c, dec_c = inp
        pair = jnp.exp(jnp.where(incl[:, :, None], b_c[:, :, :, None, :] - b_c[:, :, None, :, :], -jnp.inf))
        attn = jnp.einsum('bhid,bhijd,bhjd->bhij', q_c, pair, k_c)
        o = jnp.einsum('bhid,bhde->bhie', qd_c, state) + jnp.einsum('bhij,bhje->bhie', attn, v_c)
        state = state * dec_c + jnp.einsum('bhcd,bhce->bhde', kt_c, v_c)
        return state, o

    s0 = jnp.zeros((bsz, q.shape[2], q.shape[-1], v.shape[-1]), jnp.float32)
    _, o = lax.scan(step, s0, (q, k, v, b, q_dec, k_tail, chunk_decay))
    return from_chunks(o)


def mlstm_scan(q, k, v, log_i, log_f):
    bsz, nh = q.shape[0], q.shape[2]
    dk, dv = q.shape[-1], v.shape[-1]
    q, k, v, log_i, log_f = (to_chunks(t.astype(jnp.float32)) for t in (q, k, v, log_i, log_f))
    incl = jnp.tril(jnp.ones((CHUNK, CHUNK), dtype=bool))
    b = jnp.cumsum(log_f, axis=-1)

    def step(carry, inp):
        c_mat, n_vec, m = carry
        q_c, k_c, v_c, li_c, b_c = inp
        d = jnp.where(incl, b_c[..., :, None] - b_c[..., None, :] + li_c[..., None, :], -jnp.inf)
        inter = b_c + m[..., None]
        m_row = jnp.maximum(inter, jnp.max(d, axis=-1))
        s = jnp.einsum('bhid,bhjd->bhij', q_c, k_c) * jnp.exp(d - m_row[..., None])
        w_inter = jnp.exp(inter - m_row)
        num = w_inter[..., None] * jnp.einsum('bhid,bhde->bhie', q_c, c_mat) + jnp.einsum('bhij,bhje->bhie', s, v_c)
        den = w_inter * jnp.einsum('bhid,bhd->bhi', q_c, n_vec) + jnp.sum(s, axis=-1)
        h_out = num / jnp.maximum(jnp.abs(den), jnp.exp(-m_row))[..., None]
        d_last = b_c[..., -1:] - b_c + li_c
        inter_last = b_c[..., -1] + m
        m_new = jnp.maximum(inter_last, jnp.max(d_last, axis=-1))
        k_w = k_c * jnp.exp(d_last - m_new[..., None])[..., None]
        carry_decay = jnp.exp(inter_last - m_new)
        c_mat = carry_decay[..., None, None] * c_mat + jnp.einsum('bhcd,bhce->bhde', k_w, v_c)
        n_vec = carry_decay[..., None] * n_vec + jnp.sum(k_w, axis=2)
        return (c_mat, n_vec, m_new), h_out

    init = (jnp.zeros((bsz, nh, dk, dv), jnp.float32), jnp.zeros((bsz, nh, dk), jnp.float32),
            jnp.zeros((bsz, nh), jnp.float32))
    _, o = lax.scan(step, init, (q, k, v, log_i, b))
    return from_chunks(o)


def retention_scan(q, k, v, log_gamma):
    bsz = q.shape[0]
    q, k, v = (to_chunks(t.astype(jnp.float32)) for t in (q, k, v))
    pos = jnp.arange(CHUNK, dtype=jnp.float32)
    rel = pos[:, None] - pos[None, :]
    dmask = jnp.where(rel >= 0, jnp.exp(log_gamma[:, None, None] * jnp.maximum(rel, 0.0)), 0.0)
    q_dec = jnp.exp(log_gamma[:, None] * (pos + 1.0))[..., None]
    k_dec = jnp.exp(log_gamma[:, None] * (CHUNK - 1.0 - pos))[..., None]
    chunk_dec = jnp.exp(log_gamma * CHUNK)[:, None, None]

    def step(state, inp):
        q_c, k_c, v_c = inp
        inner = jnp.einsum('bhid,bhjd->bhij', q_c, k_c) * dmask
        o = jnp.einsum('bhid,bhde->bhie', q_c * q_dec, state) + jnp.einsum('bhij,bhje->bhie', inner, v_c)
        state = chunk_dec * state + jnp.einsum('bhcd,bhce->bhde', k_c * k_dec, v_c)
        return state, o

    s0 = jnp.zeros((bsz, q.shape[2], q.shape[-1], v.shape[-1]), jnp.float32)
    _, o = lax.scan(step, s0, (q, k, v))
    return from_chunks(o)


def hgrn_gates(z, lb):
    log_f = jnp.logaddexp(jnp.log(lb), jnp.log1p(-lb) + jax.nn.log_sigmoid(z))
    k = (1.0 - lb) * jax.nn.sigmoid(-z)
    shape = z.shape[:2] + (H_B, DK_B)
    return log_f.reshape(shape), k.reshape(shape)


def even_mixer(h, w_in, w_out, conv_w, a_log, dt_bias, norm_a, lb_fwd, lb_bwd, norm_b):
    f32 = jnp.float32
    bsz, l, _ = h.shape
    proj = jnp.einsum('bld,de->ble', h, w_in)
    qa, ka, va, ga, gate_a, qb, fb, ib, gb = split_cols(proj, EVEN_SIZES)
    qkv = jax.nn.silu(centred_conv(jnp.concatenate([qa, ka, va], axis=-1), conv_w))
    qa, ka, va = split_cols(qkv, EVEN_SIZES[:3])
    qa = l2norm(qa.astype(f32).reshape(bsz, l, H_A, DK_A)) * DK_A ** -0.5
    ka = l2norm(ka.astype(f32).reshape(bsz, l, H_A, DK_A))
    va = va.astype(f32).reshape(bsz, l, H_A, DV_A)
    gate_a = gate_a.astype(f32).reshape(bsz, l, 4, H_A)
    log_decay = -jnp.exp(a_log.astype(f32)) * jax.nn.softplus(gate_a[:, :, :2] + dt_bias.astype(f32))
    beta = jax.nn.sigmoid(gate_a[:, :, 2:])
    o_a = (gated_delta_scan(qa, ka, va, beta[:, :, 0], log_decay[:, :, 0])
           + reverse_direction(gated_delta_scan, qa, ka, va, beta[:, :, 1], log_decay[:, :, 1]))
    o_a = head_rmsnorm(o_a) * norm_a.astype(f32) * jax.nn.silu(ga.astype(f32).reshape(bsz, l, H_A, DV_A))
    qb = qb.astype(f32).reshape(bsz, l, H_B, DK_B)
    ib = ib.astype(f32).reshape(bsz, l, H_B, DV_B)
    z_fwd, z_bwd = split_cols(fb.astype(f32), (H_B * DK_B, H_B * DK_B))
    lf_fwd, k_fwd = hgrn_gates(z_fwd, lb_fwd)
    lf_bwd, k_bwd = hgrn_gates(z_bwd, lb_bwd)
    o_b = gla_scan(qb, k_fwd, ib, lf_fwd) + reverse_direction(gla_scan, qb, k_bwd, ib, lf_bwd)
    o_b = head_rmsnorm(o_b) * norm_b.astype(f32) * jax.nn.silu(gb.astype(f32).reshape(bsz, l, H_B, DV_B))
    mixed = jnp.concatenate([o_a.reshape(bsz, l, -1), o_b.reshape(bsz, l, -1)], axis=-1).astype(h.dtype)
    return jnp.einsum('ble,ed->bld', mixed, w_out)


def odd_mixer(h, w_in, w_out, ig_bias, fg_bias, norm_c):
    f32 = jnp.float32
    bsz, l, _ = h.shape
    proj = jnp.einsum('bld,de->ble', h, w_in)
    qc, kc, vc, oc, gate_c, qd, kd, vd, gd = split_cols(proj, ODD_SIZES)
    qc = qc.astype(f32).reshape(bsz, l, H_C, DK_C) * DK_C ** -0.5
    kc = kc.astype(f32).reshape(bsz, l, H_C, DK_C)
    vc = vc.astype(f32).reshape(bsz, l, H_C, DV_C)
    gate_c = gate_c.astype(f32).reshape(bsz, l, 4, H_C)
    log_i = gate_c[:, :, :2] + ig_bias.astype(f32)
    log_f = jax.nn.log_sigmoid(gate_c[:, :, 2:] + fg_bias.astype(f32))
    h_c = (mlstm_scan(qc, kc, vc, log_i[:, :, 0], log_f[:, :, 0])
           + reverse_direction(mlstm_scan, qc, kc, vc, log_i[:, :, 1], log_f[:, :, 1]))
    h_c = jax.nn.sigmoid(oc.astype(f32).reshape(bsz, l, H_C, DV_C)) * (head_rmsnorm(h_c) * norm_c.astype(f32))
    qd = rotary(qd.astype(f32).reshape(bsz, l, H_D, DK_D)) * DK_D ** -0.5
    kd = rotary(kd.astype(f32).reshape(bsz, l, H_D, DK_D))
    vd = vd.astype(f32).reshape(bsz, l, H_D, DV_D)
    lg_fwd = jnp.log1p(-jnp.exp2(-RET_DECAY_MIN_EXP - jnp.arange(H_D, dtype=f32)))
    lg_bwd = lg_fwd[::-1]
    o_d = (retention_scan(qd, kd, vd, lg_fwd)
           + reverse_direction(functools.partial(retention_scan, log_gamma=lg_bwd), qd, kd, vd))
    o_d = head_rmsnorm(o_d) * jax.nn.silu(gd.astype(f32).reshape(bsz, l, H_D, DV_D))
    mixed = jnp.concatenate([h_c.reshape(bsz, l, -1), o_d.reshape(bsz, l, -1)], axis=-1).astype(h.dtype)
    return jnp.einsum('ble,ed->bld', mixed, w_out)


def sq_relu_mlp(h, w_up, w_down):
    a = jax.nn.relu(jnp.einsum('bld,df->blf', h, w_up))
    return jnp.einsum('blf,fd->bld', a * a, w_down)


def trunk(x, norm_mix, norm_mlp, norm_final, w_in_even, w_out_even, conv_a, a_log_a, dt_bias_a, norm_a,
          lb_b, norm_b, w_in_odd, w_out_odd, ig_bias_c, fg_bias_c, norm_c, w_up, w_down):
    p = jax.nn.softmax(lb_b.astype(jnp.float32), axis=1)
    lower_bounds = jnp.cumsum(p, axis=1) - p[:, :1]
    for layer in range(DEPTH):
        i = layer // 2
        h = rmsnorm(x, norm_mix[layer])
        if layer % 2 == 0:
            x = x + even_mixer(h, w_in_even[i], w_out_even[i], conv_a[i], a_log_a[i], dt_bias_a[i], norm_a[i],
                               lower_bounds[0, i], lower_bounds[1, i], norm_b[i])
        else:
            x = x + odd_mixer(h, w_in_odd[i], w_out_odd[i], ig_bias_c[i], fg_bias_c[i], norm_c[i])
        x = x + sq_relu_mlp(rmsnorm(x, norm_mlp[layer]), w_up[layer], w_down[layer])
    return rmsnorm(x, norm_final)


def setup_inputs(seed: int = 0) -> dict:
    key = jax.random.key(seed)
    ks = jax.random.split(key, 20)
    f32 = jnp.float32

    def dense(k, shape, fan_in):
        return jax.random.normal(k, shape, f32) * fan_in ** -0.5

    def gain(k, shape):
        return 1.0 + 0.02 * jax.random.normal(k, shape, f32)

    dt = jnp.exp(jax.random.uniform(ks[9], (N_EVEN, 2, H_A), f32, math.log(1e-3), math.log(1e-1)))
    return {
        'x_prompt': jax.random.normal(ks[0], (BATCH, SEQ, D_MODEL), f32),
        'x_sample': jax.random.normal(ks[1], (DEC_BATCH, DEC_SEQ, D_MODEL), f32),
        'norm_mix': gain(ks[2], (DEPTH, D_MODEL)),
        'norm_mlp': gain(ks[3], (DEPTH, D_MODEL)),
        'norm_final': gain(ks[4], (D_MODEL,)),
        'w_in_even': dense(ks[5], (N_EVEN, D_MODEL, EVEN_COLS), D_MODEL),
        'w_out_even': dense(ks[6], (N_EVEN, EVEN_MIX, D_MODEL), EVEN_MIX),
        'conv_a': dense(ks[7], (N_EVEN, CONV_W, CONV_CH), CONV_W),
        'a_log_a': jnp.log(jax.random.uniform(ks[8], (N_EVEN, 2, H_A), f32, 1.0, 16.0)),
        'dt_bias_a': dt + jnp.log(-jnp.expm1(-dt)),
        'norm_a': gain(ks[10], (N_EVEN, DV_A)),
        'lb_b': 0.5 * jax.random.normal(ks[11], (2, N_EVEN, H_B * DK_B), f32),
        'norm_b': gain(ks[12], (N_EVEN, DV_B)),
        'w_in_odd': dense(ks[13], (N_ODD, D_MODEL, ODD_COLS), D_MODEL),
        'w_out_odd': dense(ks[14], (N_ODD, ODD_MIX, D_MODEL), ODD_MIX),
        'ig_bias_c': -2.0 + 0.5 * jax.random.normal(ks[15], (N_ODD, 2, H_C), f32),
        'fg_bias_c': jnp.linspace(3.0, 6.0, H_C, dtype=f32) + 0.1 * jax.random.normal(ks[16], (N_ODD, 2, H_C), f32),
        'norm_c': gain(ks[17], (N_ODD, DV_C)),
        'w_up': dense(ks[18], (DEPTH, D_MODEL, D_FF), D_MODEL),
        'w_down': dense(ks[19], (DEPTH, D_FF, D_MODEL), D_FF),
    }


def reference(x_prompt, x_sample, norm_mix, norm_mlp, norm_final, w_in_even, w_out_even, conv_a, a_log_a,
              dt_bias_a, norm_a, lb_b, norm_b, w_in_odd, w_out_odd, ig_bias_c, fg_bias_c, norm_c, w_up, w_down):
    weights = (norm_mix, norm_mlp, norm_final, w_in_even, w_out_even, conv_a, a_log_a, dt_bias_a, norm_a,
               lb_b, norm_b, w_in_odd, w_out_odd, ig_bias_c, fg_bias_c, norm_c, w_up, w_down)
    y_prompt = trunk(x_prompt, *weights)
    y_sample = trunk(x_sample, *weights)
    return (y_prompt, y_sample)
```

```python
import contextlib
import math
import numpy as np
import concourse.bass as bass
import concourse.mybir as mybir
from concourse.bass_utils import run_bass_kernel_spmd

F32 = mybir.dt.float32
BF16 = mybir.dt.bfloat16
I32 = mybir.dt.int32
AF = mybir.ActivationFunctionType
ALU = mybir.AluOpType

D = 4096
DFF = 16384
KC = 32
TT = 512
C = 128
EPS = 1e-6
EVEN_COLS = 18496
ODD_COLS = 14352


class Buf:
    __slots__ = ("w", "r", "dsem", "dcnt", "name")

    def __init__(self, name=""):
        self.w = {}
        self.r = {}
        self.dsem = None
        self.dcnt = 0
        self.name = name


class Eng:
    def __init__(self, e, name, sem_idx):
        self.e = e
        self.name = name
        self.sem = sem_idx
        self.cnt = 0
        self.waited = {}


class Prog:
    def __init__(self, nc):
        self.nc = nc
        self.stack = contextlib.ExitStack()
        self.sems = []
        self.semval = []
        self.pe = self._eng(nc.tensor, "pe")
        self.act = self._eng(nc.scalar, "act")
        self.dve = self._eng(nc.vector, "dve")
        self.pool = self._eng(nc.gpsimd, "pool")
        self.sp = self._eng(nc.sync, "sp")
        self.engs = [self.pe, self.act, self.dve, self.pool, self.sp]
        self.dsem_pool = {}
        self.n_ins = 0

    def new_sem(self, name):
        s = self.stack.enter_context(self.nc.semaphore(name))
        self.sems.append(s)
        self.semval.append(0)
        return len(self.sems) - 1

    def _eng(self, e, name):
        return Eng(e, name, self.new_sem("s_" + name))

    def _wait(self, eng, deps, skip_own=True):
        for sem, val in deps.items():
            if skip_own and sem == eng.sem:
                continue
            if eng.waited.get(sem, 0) < val:
                eng.e.wait_ge(self.sems[sem], val)
                eng.waited[sem] = val

    @staticmethod
    def _merge(d, s):
        for k, v in s.items():
            if d.get(k, 0) < v:
                d[k] = v

    def op(self, eng, fn, reads=(), writes=(), inc=True):
        deps = {}
        for b in reads:
            self._merge(deps, b.w)
        for b in writes:
            self._merge(deps, b.w)
            self._merge(deps, b.r)
        self._wait(eng, deps, skip_own=(eng is self.pe))
        ins = fn()
        self.n_ins += 1
        if inc:
            eng.cnt += 1
            ins.then_inc(self.sems[eng.sem], 1)
            self.semval[eng.sem] = eng.cnt
            tok = eng.cnt
        else:
            tok = eng.cnt + 1
        for b in reads:
            if b.r.get(eng.sem, 0) < tok:
                b.r[eng.sem] = tok
        for b in writes:
            b.w = {eng.sem: tok}
            b.r = {}
        return ins

    def dsem_for(self, key):
        if key not in self.dsem_pool:
            self.dsem_pool[key] = [self.new_sem("d_" + key), 0]
        return self.dsem_pool[key]

    def dma(self, q, out, in_, reads=(), writes=(), key=None, disjoint=False):
        deps = {}
        for b in reads:
            self._merge(deps, b.w)
        for b in writes:
            if not disjoint:
                self._merge(deps, b.w)
            self._merge(deps, b.r)
        self._wait(q, deps, skip_own=False)
        ent = self.dsem_for(key)
        sem = ent[0]
        ent[1] += 16
        tok = ent[1]
        self.semval[sem] = tok
        q.e.dma_start(out=out, in_=in_).then_inc(self.sems[sem], 16)
        self.n_ins += 1
        for b in reads:
            if b.r.get(sem, 0) < tok:
                b.r[sem] = tok
        for b in writes:
            if disjoint:
                if b.w.get(sem, 0) < tok:
                    b.w[sem] = tok
            else:
                b.w = {sem: tok}
                b.r = {}

    def barrier(self):
        deps = {i: v for i, v in enumerate(self.semval) if v > 0}
        for eng in self.engs:
            self._wait(eng, deps, skip_own=True)


def col_plan(even):
    if even:
        Fg = [("qa", 0, 2048), ("ka", 2048, 2048), ("va", 4096, 2048), ("qb", 8256, 2048),
              ("zf", 10304, 2048), ("zb", 12352, 2048)]
        Tg = [("ga", 6144, 2048), ("ib", 14400, 2048), ("gb", 16448, 2048), ("gate", 8192, 64)]
    else:
        Fg = [("qc", 0, 1024), ("kc", 1024, 1024), ("qd", 6160, 2048), ("kd", 8208, 2048)]
        Tg = [("vc", 2048, 2048), ("oc", 4096, 2048), ("vd", 10256, 2048), ("gd", 12304, 2048),
              ("gate", 6144, 16)]
    return Fg, Tg


def blocks_of(groups):
    out = []
    off = 0
    for _, c0, w in groups:
        for b in range(0, w, 512):
            nb = min(512, w - b)
            out.append((c0 + b, nb, off + b))
        off += w
    return out, off


def group_off(groups, name):
    off = 0
    for n, _, w in groups:
        if n == name:
            return off
        off += w
    raise KeyError(name)


class Builder:
    def __init__(self, seqs, depth, debug_io=False):
        self.debug_io = debug_io
        self.seqs = list(seqs)
        self.depth = depth
        self.T = sum(seqs)
        assert all(s % TT == 0 for s in seqs)
        self.nc = bass.Bass("TRN2", target_bir_lowering=False)
        self.P = Prog(self.nc)
        self.dram = {}
        self.dbuf = {}

    def din(self, name, shape, dt=F32):
        t = self.nc.dram_tensor(name, list(shape), dt, kind="ExternalInput").ap()
        self.dram[name] = t
        self.dbuf[name] = Buf(name)
        return t

    def dout(self, name, shape, dt=F32):
        t = self.nc.dram_tensor(name, list(shape), dt, kind="ExternalOutput").ap()
        self.dram[name] = t
        self.dbuf[name] = Buf(name)
        return t

    def dint(self, name, shape, dt=F32):
        t = self.nc.dram_tensor(name, list(shape), dt, kind="Internal").ap()
        self.dram[name] = t
        self.dbuf[name] = Buf(name)
        return t

    def seq_of(self, tok):
        t0 = 0
        for si, L in enumerate(self.seqs):
            if tok < t0 + L:
                return si, tok - t0
            t0 += L
        raise ValueError(tok)

    def PF(self, rows0, nrows, tok, ntok):
        si, lt = self.seq_of(tok)
        band, r = rows0 // 2048, rows0 % 2048
        assert r + nrows <= 2048 and lt + ntok <= self.seqs[si]
        return self.pf[si][band][r:r + nrows, lt:lt + ntok]

    def PT(self, tok, ntok, col0, ncols):
        si, lt = self.seq_of(tok)
        band, c = col0 // 2048, col0 % 2048
        assert c + ncols <= 2048 and lt + ntok <= self.seqs[si]
        return self.pt[si][band][lt:lt + ntok, c:c + ncols]

    def sb(self, st, name, shape, dt=F32):
        self.uid = getattr(self, "uid", 0) + 1
        return st.enter_context(self.nc.sbuf_tensor(f"sb{self.uid}_{name}", list(shape), dt))

    def ps(self, st, name, shape, dt=F32):
        self.uid = getattr(self, "uid", 0) + 1
        return st.enter_context(self.nc.psum_tensor(f"ps{self.uid}_{name}", list(shape), dt))

    def declare(self):
        T, depth = self.T, self.depth
        ne, no = (depth + 1) // 2, depth // 2
        self.din("xT", [D, T])
        self.dout("yT", [D, T])
        self.din("w_in_even", [ne, D, EVEN_COLS])
        self.din("w_out_even", [ne, D, D])
        if no:
            self.din("w_in_odd", [no, D, ODD_COLS])
            self.din("w_out_odd", [no, D, D])
        self.din("w_up", [depth, D, DFF])
        self.din("w_down", [depth, DFF, D])
        self.din("gains", [128, (2 * depth + 1) * KC])
        self.din("ident", [128, 128])
        self.dint("xres", [D, T])
        self.wb = {}
        for l in range(depth):
            even = (l % 2 == 0)
            Fg, Tg = col_plan(even)
            fb, nF = blocks_of(Fg)
            tb, nT = blocks_of(Tg)
            self.wb[("in", l)] = self.dint(f"wb_in{l}", [len(fb) + len(tb), 128, KC * 512], BF16)
            self.wb[("out", l)] = self.dint(f"wb_out{l}", [8, 128, KC * 512], BF16)
            self.wb[("up", l)] = self.dint(f"wb_up{l}", [32, 128, KC * 512], BF16)
            self.wb[("down", l)] = self.dint(f"wb_down{l}", [32, 128, 16 * 1024], BF16)
        self.pf, self.pt = [], []
        for si, L in enumerate(self.seqs):
            self.pf.append([self.dint(f"PF_{si}_{b}", [2048, L]) for b in range(6)])
            self.pt.append([self.dint(f"PT_{si}_{b}", [L, 2048]) for b in range(5)])
        self.dbuf["PF"] = Buf("PF")
        self.dbuf["PT"] = Buf("PT")
        if self.debug_io:
            self.dout("mixed", [T, D], BF16)
        else:
            self.dint("mixed", [T, D], BF16)
        self.dint("OT", [T, D])

    def prologue(self):
        nc, P = self.nc, self.P
        units = []
        for l in range(self.depth):
            even = (l % 2 == 0)
            i = l // 2
            Fg, Tg = col_plan(even)
            fb, _ = blocks_of(Fg)
            tb, _ = blocks_of(Tg)
            wname = "w_in_even" if even else "w_in_odd"
            w = self.dram[wname][i]
            for bi, (c0, nb, _) in enumerate(fb + tb):
                src = w[:, c0:c0 + nb].rearrange("(kc p) c -> p kc c", p=128)
                dst = self.wb[("in", l)][bi]
                for h in range(2):
                    units.append((src[:, h * 16:(h + 1) * 16, :], dst[:, h * 16 * nb:(h + 1) * 16 * nb], [16, nb], wname))
            oname = "w_out_even" if even else "w_out_odd"
            w = self.dram[oname][i]
            for bi in range(8):
                src = w[:, bi * 512:(bi + 1) * 512].rearrange("(kc p) c -> p kc c", p=128)
                dst = self.wb[("out", l)][bi]
                for h in range(2):
                    units.append((src[:, h * 16:(h + 1) * 16, :], dst[:, h * 8192:(h + 1) * 8192], [16, 512], oname))
            w = self.dram["w_up"][l]
            for bi in range(32):
                src = w[:, bi * 512:(bi + 1) * 512].rearrange("(kc p) c -> p kc c", p=128)
                dst = self.wb[("up", l)][bi]
                for h in range(2):
                    units.append((src[:, h * 16:(h + 1) * 16, :], dst[:, h * 8192:(h + 1) * 8192], [16, 512], "w_up"))
            w = self.dram["w_down"][l]
            for fbk in range(8):
                for db in range(4):
                    src = w[fbk * 2048:(fbk + 1) * 2048, db * 1024:(db + 1) * 1024].rearrange("(kc p) c -> p kc c", p=128)
                    dst = self.wb[("down", l)][fbk * 4 + db]
                    for h in range(2):
                        units.append((src[:, h * 8:(h + 1) * 8, :], dst[:, h * 8192:(h + 1) * 8192], [8, 1024], "w_down"))
        with contextlib.ExitStack() as st:
            NS = 3
            stg = [self.sb(st, f"pl_in{k}", [128, 8192], F32) for k in range(NS)]
            stgb = [Buf(f"pl_in{k}") for k in range(NS)]
            outb = [self.sb(st, f"pl_out{k}", [128, 8192], BF16) for k in range(NS)]
            outbb = [Buf(f"pl_out{k}") for k in range(NS)]
            wdst = Buf("wdst")
            for u, (src, dst, shp, sname) in enumerate(units):
                k = u % NS
                n = shp[0] * shp[1]
                sv = stg[k][:, 0:n].rearrange("p (a b) -> p a b", a=shp[0])
                P.dma(P.sp, sv, src, reads=[self.dbuf[sname]], writes=[stgb[k]], key=f"pl_in{k}")
                eng = [P.dve, P.act, P.pool][u % 3]
                if eng is P.dve:
                    P.op(eng, lambda: nc.vector.tensor_copy(out=outb[k][:, 0:n], in_=stg[k][:, 0:n]), reads=[stgb[k]], writes=[outbb[k]])
                elif eng is P.act:
                    P.op(eng, lambda: nc.scalar.copy(out=outb[k][:, 0:n], in_=stg[k][:, 0:n]), reads=[stgb[k]], writes=[outbb[k]])
                else:
                    P.op(eng, lambda: nc.gpsimd.tensor_copy(out=outb[k][:, 0:n], in_=stg[k][:, 0:n]), reads=[stgb[k]], writes=[outbb[k]])
                P.dma(P.pool, dst, outb[k][:, 0:n], reads=[outbb[k]], writes=[wdst], key=f"pl_out{k}", disjoint=True)
            P.barrier()

    def dense_phase(self, prev, nxt):
        nc, P = self.nc, self.P
        T = self.T
        ntile = T // TT
        with contextlib.ExitStack() as st:
            xt = self.sb(st, "xt", [128, KC, TT], F32)
            xtb = [Buf(f"xt{k}") for k in range(KC)]
            ht = self.sb(st, "ht", [128, KC, TT], BF16)
            htb = Buf("ht")
            NSLOT = 2
            ring = [self.sb(st, f"ring{k}", [128, KC * 512], BF16) for k in range(NSLOT)]
            ringb = [Buf(f"ring{k}") for k in range(NSLOT)]
            at = self.sb(st, "at", [128, 16, TT], BF16)
            atb = [Buf(f"at{k}") for k in range(16)]
            mx = [self.sb(st, f"mx{k}", [128, 2048], BF16) for k in range(2)]
            mxb = [Buf(f"mx{k}") for k in range(2)]
            NSTG = 3
            stg = [self.sb(st, f"stg{k}", [128, 512], F32) for k in range(NSTG)]
            stgb = [Buf(f"stg{k}") for k in range(NSTG)]
            gains = self.sb(st, "gains", [128, (2 * self.depth + 1) * KC], F32)
            gainsb = Buf("gains")
            ident = self.sb(st, "identf", [128, 128], F32)
            identb = Buf("identf")
            identbf = self.sb(st, "identbf", [128, 128], BF16)
            identbfb = Buf("identbf")
            onesbf = self.sb(st, "onesbf", [128, 128], BF16)
            onesbfb = Buf("onesbf")
            rstd = self.sb(st, "rstd", [128, TT], F32)
            rstdb = Buf("rstd")
            relu = [self.sb(st, f"relu{k}", [128, TT], F32) for k in range(2)]
            relub = [Buf(f"relu{k}") for k in range(2)]
            NPS = 6
            pst = [self.ps(st, f"dps{k}", [128, 512], F32) for k in range(NPS)]
            pstb = [Buf(f"dps{k}") for k in range(NPS)]
            ptr = [self.ps(st, f"dpt{k}", [128, 1024], BF16) for k in range(2)]
            ptrb = [Buf(f"dpt{k}") for k in range(2)]
            cnt = {"ps": 0, "stg": 0, "ev": 0, "ring": 0, "relu": 0, "ptr": 0}

            P.dma(P.sp, gains[:], self.dram["gains"][:, :], reads=[self.dbuf["gains"]], writes=[gainsb], key="c_gains")
            P.dma(P.sp, ident[:], self.dram["ident"][:, :], reads=[self.dbuf["ident"]], writes=[identb], key="c_ident")
            P.op(P.dve, lambda: nc.vector.tensor_copy(out=identbf[:], in_=ident[:]), reads=[identb], writes=[identbfb])
            P.op(P.pool, lambda: nc.gpsimd.memset(onesbf[:], 1.0), writes=[onesbfb])

            def tile_blocks():
                bl = []
                if prev is not None:
                    for bi in range(8):
                        bl.append((self.wb[("out", prev)][bi], f"wb_out{prev}"))
                    for fbk in range(8):
                        for j in range(4):
                            bl.append((self.wb[("up", prev)][fbk * 4 + j], f"wb_up{prev}"))
                        for db in range(4):
                            bl.append((self.wb[("down", prev)][fbk * 4 + db], f"wb_down{prev}"))
                if nxt is not None:
                    Fg, Tg = col_plan(nxt % 2 == 0)
                    fb, _ = blocks_of(Fg)
                    tb, _ = blocks_of(Tg)
                    for bi, (c0, nb, off) in enumerate(fb + tb):
                        bl.append((self.wb[("in", nxt)][bi][:, 0:KC * nb], f"wb_in{nxt}"))
                return bl

            blist = tile_blocks() * ntile
            state = {"next_load": 0, "next_use": 0}

            def issue_load():
                i = state["next_load"]
                if i >= len(blist):
                    return
                ap, nm = blist[i]
                k = i % NSLOT
                n = ap.shape[1]
                P.dma(P.sp, ring[k][:, 0:n], ap, reads=[self.dbuf[nm]], writes=[ringb[k]], key=f"ring{k}")
                state["next_load"] = i + 1

            def get_block():
                i = state["next_use"]
                while state["next_load"] <= min(i + NSLOT - 1, len(blist) - 1):
                    issue_load()
                state["next_use"] = i + 1
                k = i % NSLOT
                return ring[k], ringb[k]

            def next_ps():
                k = cnt["ps"] % NPS
                cnt["ps"] += 1
                return pst[k], pstb[k]

            def ev_engine():
                cnt["ev"] += 1
                return P.act if cnt["ev"] % 2 else P.dve

            def copy_on(eng, out, in_, reads, writes):
                if eng is P.act:
                    P.op(eng, lambda: nc.scalar.copy(out=out, in_=in_), reads=reads, writes=writes)
                elif eng is P.dve:
                    P.op(eng, lambda: nc.vector.tensor_copy(out=out, in_=in_), reads=reads, writes=writes)
                else:
                    P.op(eng, lambda: nc.gpsimd.tensor_copy(out=out, in_=in_), reads=reads, writes=writes)

            def rmsnorm(gidx):
                for kc in range(KC):
                    P.op(P.act, lambda: nc.scalar.activation(out=ht[:, kc, :], in_=xt[:, kc, :], func=AF.Square),
                         reads=[xtb[kc]], writes=[htb])
                p, pb = next_ps()
                for kc in range(KC):
                    P.op(P.pe, lambda: nc.tensor.matmul(p[:], lhsT=onesbf[:], rhs=ht[:, kc, :], start=(kc == 0), stop=(kc == KC - 1)),
                         reads=[onesbfb, htb], writes=[pb], inc=(kc == KC - 1))
                P.op(P.act, lambda: nc.scalar.activation(out=rstd[:], in_=p[:], func=AF.Ln, bias=EPS, scale=1.0 / D),
                     reads=[pb], writes=[rstdb])
                P.op(P.act, lambda: nc.scalar.activation(out=rstd[:], in_=rstd[:], func=AF.Exp, scale=-0.5),
                     reads=[rstdb], writes=[rstdb])
                for kc in range(KC):
                    g = gains[:, gidx * KC + kc:gidx * KC + kc + 1]
                    P.op(P.dve, lambda: nc.vector.scalar_tensor_tensor(out=ht[:, kc, :], in0=xt[:, kc, :], scalar=g, in1=rstd[:],
                                                                       op0=ALU.mult, op1=ALU.mult),
                         reads=[xtb[kc], rstdb, gainsb], writes=[htb])

            tok0 = 0
            for t in range(ntile):
                tok0 = t * TT
                src = "xT" if (prev is None or prev == 0) else "xres"
                xv = self.dram[src][:, tok0:tok0 + TT].rearrange("(kc p) t -> p kc t", p=128)
                for q4 in range(4):
                    P.dma(P.sp, xt[:, q4 * 8:(q4 + 1) * 8, :], xv[:, q4 * 8:(q4 + 1) * 8, :], reads=[self.dbuf[src]],
                          writes=xtb[q4 * 8:(q4 + 1) * 8], key=f"xt{q4}")
                if prev is not None:
                    for s in range(4):
                        for hh in range(2):
                            k = (s * 2 + hh) % 2
                            P.dma(P.sp, mx[k][:], self.dram["mixed"][tok0 + s * 128:tok0 + (s + 1) * 128, hh * 2048:(hh + 1) * 2048],
                                  reads=[self.dbuf["mixed"]], writes=[mxb[k]], key=f"mx{k}")
                            for g8 in range(2):
                                pk = cnt["ptr"] % 2
                                cnt["ptr"] += 1
                                for j in range(8):
                                    cc = g8 * 8 + j
                                    P.op(P.pe, lambda: nc.tensor.transpose(ptr[pk][:, j * 128:(j + 1) * 128], mx[k][:, cc * 128:(cc + 1) * 128], identbf[:]),
                                         reads=[mxb[k], identbfb], writes=[ptrb[pk]], inc=(j == 7))
                                kc0 = hh * 16 + g8 * 8
                                eng = ev_engine()
                                copy_on(eng, ht[:, kc0:kc0 + 8, s * 128:(s + 1) * 128],
                                        ptr[pk][:].rearrange("p (j c) -> p j c", j=8), [ptrb[pk]], [htb])
                    for bi in range(8):
                        blk, blkb = get_block()
                        for cc in range(4):
                            p, pb = next_ps()
                            for kc in range(KC):
                                P.op(P.pe, lambda: nc.tensor.matmul(p[:], lhsT=blk[:, kc * 512 + cc * 128:kc * 512 + (cc + 1) * 128], rhs=ht[:, kc, :],
                                                                    start=(kc == 0), stop=(kc == KC - 1)),
                                     reads=[blkb, htb], writes=[pb], inc=(kc == KC - 1))
                            dc = bi * 4 + cc
                            P.op(P.dve, lambda: nc.vector.tensor_tensor(out=xt[:, dc, :], in0=xt[:, dc, :], in1=p[:], op=ALU.add),
                                 reads=[pb, xtb[dc]], writes=[xtb[dc]])
                    rmsnorm(self.depth + prev)
                    for fbk in range(8):
                        for j in range(4):
                            blk, blkb = get_block()
                            for cc in range(4):
                                p, pb = next_ps()
                                for kc in range(KC):
                                    P.op(P.pe, lambda: nc.tensor.matmul(p[:], lhsT=blk[:, kc * 512 + cc * 128:kc * 512 + (cc + 1) * 128], rhs=ht[:, kc, :],
                                                                        start=(kc == 0), stop=(kc == KC - 1)),
                                         reads=[blkb, htb], writes=[pb], inc=(kc == KC - 1))
                                jj = j * 4 + cc
                                rk = cnt["relu"] % 2
                                cnt["relu"] += 1
                                P.op(P.act, lambda: nc.scalar.activation(out=relu[rk][:], in_=p[:], func=AF.Relu), reads=[pb], writes=[relub[rk]])
                                P.op(P.pool, lambda: nc.gpsimd.tensor_tensor(out=at[:, jj, :], in0=relu[rk][:], in1=relu[rk][:], op=ALU.mult),
                                     reads=[relub[rk]], writes=[atb[jj]])
                        for db in range(4):
                            blk, blkb = get_block()
                            for cc in range(8):
                                p, pb = next_ps()
                                for j in range(16):
                                    P.op(P.pe, lambda: nc.tensor.matmul(p[:], lhsT=blk[:, j * 1024 + cc * 128:j * 1024 + (cc + 1) * 128], rhs=at[:, j, :],
                                                                        start=(j == 0), stop=(j == 15)),
                                         reads=[blkb, atb[j]], writes=[pb], inc=(j == 15))
                                dc = db * 8 + cc
                                P.op(P.dve, lambda: nc.vector.tensor_tensor(out=xt[:, dc, :], in0=xt[:, dc, :], in1=p[:], op=ALU.add),
                                     reads=[pb, xtb[dc]], writes=[xtb[dc]])
                if nxt is not None:
                    if prev is not None:
                        xo = self.dram["xres"][:, tok0:tok0 + TT].rearrange("(kc p) t -> p kc t", p=128)
                        for q4 in range(4):
                            P.dma(P.pool, xo[:, q4 * 8:(q4 + 1) * 8, :], xt[:, q4 * 8:(q4 + 1) * 8, :], reads=xtb[q4 * 8:(q4 + 1) * 8],
                                  writes=[self.dbuf["xres"]], key=f"xo{q4}", disjoint=True)
                    rmsnorm(nxt)
                    Fg, Tg = col_plan(nxt % 2 == 0)
                    fb, _ = blocks_of(Fg)
                    tb, _ = blocks_of(Tg)
                    for (c0, nb, off) in fb:
                        blk, blkb = get_block()
                        for cc in range(nb // 128):
                            p, pb = next_ps()
                            for kc in range(KC):
                                P.op(P.pe, lambda: nc.tensor.matmul(p[:], lhsT=blk[:, kc * nb + cc * 128:kc * nb + (cc + 1) * 128], rhs=ht[:, kc, :],
                                                                    start=(kc == 0), stop=(kc == KC - 1)),
                                     reads=[blkb, htb], writes=[pb], inc=(kc == KC - 1))
                            sk = cnt["stg"] % NSTG
                            cnt["stg"] += 1
                            copy_on(ev_engine(), stg[sk][:], p[:], [pb], [stgb[sk]])
                            r0 = off + cc * 128
                            P.dma(P.pool, self.PF(r0, 128, tok0, TT), stg[sk][:], reads=[stgb[sk]],
                                  writes=[self.dbuf["PF"]], key=f"stg{sk}", disjoint=True)
                    for (c0, nb, off) in tb:
                        blk, blkb = get_block()
                        for s in range(4):
                            p, pb = next_ps()
                            for kc in range(KC):
                                P.op(P.pe, lambda: nc.tensor.matmul(p[:, 0:nb], lhsT=ht[:, kc, s * 128:(s + 1) * 128], rhs=blk[:, kc * nb:(kc + 1) * nb],
                                                                    start=(kc == 0), stop=(kc == KC - 1)),
                                     reads=[blkb, htb], writes=[pb], inc=(kc == KC - 1))
                            sk = cnt["stg"] % NSTG
                            cnt["stg"] += 1
                            copy_on(ev_engine(), stg[sk][:, 0:nb], p[:, 0:nb], [pb], [stgb[sk]])
                            P.dma(P.pool, self.PT(tok0 + s * 128, 128, off, nb), stg[sk][:, 0:nb], reads=[stgb[sk]],
                                  writes=[self.dbuf["PT"]], key=f"stg{sk}", disjoint=True)
                else:
                    gidx = 2 * self.depth
                    for kc in range(KC):
                        P.op(P.act, lambda: nc.scalar.activation(out=ht[:, kc, :], in_=xt[:, kc, :], func=AF.Square),
                             reads=[xtb[kc]], writes=[htb])
                    p, pb = next_ps()
                    for kc in range(KC):
                        P.op(P.pe, lambda: nc.tensor.matmul(p[:], lhsT=onesbf[:], rhs=ht[:, kc, :], start=(kc == 0), stop=(kc == KC - 1)),
                             reads=[onesbfb, htb], writes=[pb], inc=(kc == KC - 1))
                    P.op(P.act, lambda: nc.scalar.activation(out=rstd[:], in_=p[:], func=AF.Ln, bias=EPS, scale=1.0 / D), reads=[pb], writes=[rstdb])
                    P.op(P.act, lambda: nc.scalar.activation(out=rstd[:], in_=rstd[:], func=AF.Exp, scale=-0.5), reads=[rstdb], writes=[rstdb])
                    yo = self.dram["yT"][:, tok0:tok0 + TT].rearrange("(kc p) t -> p kc t", p=128)
                    for kc in range(KC):
                        g = gains[:, gidx * KC + kc:gidx * KC + kc + 1]
                        P.op(P.dve, lambda: nc.vector.scalar_tensor_tensor(out=xt[:, kc, :], in0=xt[:, kc, :], scalar=g, in1=rstd[:],
                                                                           op0=ALU.mult, op1=ALU.mult),
                             reads=[xtb[kc], rstdb, gainsb], writes=[xtb[kc]])
                    for q4 in range(4):
                        P.dma(P.pool, yo[:, q4 * 8:(q4 + 1) * 8, :], xt[:, q4 * 8:(q4 + 1) * 8, :], reads=xtb[q4 * 8:(q4 + 1) * 8],
                              writes=[self.dbuf["yT"]], key=f"xo{q4}", disjoint=True)
            P.barrier()


NEG = -30000.0


def host_consts():
    p = np.arange(128)[:, None]
    f = np.arange(128)[None, :]
    mats = {
        "ident": (p == f), "ones": np.ones((128, 128)),
        "TRIf": (p <= f), "TRIb": (p >= f), "STRf": (p > f), "STRb": (p < f),
    }
    out = {k: v.astype(np.float32) for k, v in mats.items()}
    out["UINf"] = np.where(p <= f, 0.0, NEG).astype(np.float32)
    out["UINb"] = np.where(p >= f, 0.0, NEG).astype(np.float32)
    out["LSTf"] = np.where(f < p, 0.0, NEG).astype(np.float32)
    out["LSTb"] = np.where(f > p, 0.0, NEG).astype(np.float32)
    names = list(out.keys())
    arr = np.concatenate([out[k] for k in names], axis=1)
    msk = np.concatenate([(p <= f), (p >= f)], axis=1).astype(np.int32)
    return names, arr, msk


CST_NAMES = host_consts()[0]


class Mixer:
    def __init__(self, B, layer):
        self.B = B
        self.layer = layer
        self.even = (layer % 2 == 0)
        self.i = layer // 2

    def run(self):
        B = self.B
        nc, P = B.nc, B.P
        self.nc, self.P = nc, P
        with contextlib.ExitStack() as st:
            self.st = st
            sb = lambda n, s, d=F32: B.sb(st, n, s, d)
            self.cst = sb("cst", [128, len(CST_NAMES) * 128])
            self.cstb = Buf("cst")
            P.dma(P.sp, self.cst[:], B.dram["cst"][:, :], reads=[B.dbuf["cst"]], writes=[self.cstb], key="c_cst")
            self.msk = sb("msk", [128, 256], I32)
            self.mskb = Buf("msk")
            P.dma(P.sp, self.msk[:], B.dram["msk"][:, :], reads=[B.dbuf["msk"]], writes=[self.mskb], key="c_msk")
            self.identbf = sb("identbf", [128, 128], BF16)
            self.identbfb = Buf("identbf")
            P.op(P.dve, lambda: nc.vector.tensor_copy(out=self.identbf[:], in_=self.C("ident")), reads=[self.cstb], writes=[self.identbfb])
            self.onesbf = sb("onesbf", [128, 128], BF16)
            self.onesbfb = Buf("onesbf")
            P.op(P.dve, lambda: nc.vector.tensor_copy(out=self.onesbf[:], in_=self.C("ones")), reads=[self.cstb], writes=[self.onesbfb])
            self.pst = [B.ps(st, f"mps{k}", [128, 512], F32) for k in range(7)]
            self.pstb = [Buf(f"mps{k}") for k in range(7)]
            self.ptb = B.ps(st, "mpt", [128, 1024], BF16)
            self.ptbb = Buf("mpt")
            self.pscnt = 0
            self.tmpcnt = {}
            t0 = 0
            if not self.even and "D" not in getattr(B, "skip", set()):
                self.rotary()
            for L in B.seqs:
                skip = getattr(B, "skip", set())
                if self.even:
                    if "A" not in skip:
                        self.gdn(t0, L)
                    P.barrier()
                    if "B" not in skip:
                        self.hgrn(t0, L)
                else:
                    if "C" not in skip:
                        self.mlstm_ret(t0, L, "C")
                    P.barrier()
                    if "D" not in skip:
                        self.mlstm_ret(t0, L, "D")
                P.barrier()
                t0 += L
            P.barrier()

    def C(self, name):
        k = CST_NAMES.index(name)
        return self.cst[:, k * 128:(k + 1) * 128]

    def nps(self):
        k = self.pscnt % 7
        self.pscnt += 1
        return self.pst[k], self.pstb[k]

    def tmp(self, st, name, shape, dt=F32, n=2):
        key = name
        if key not in self.tmpcnt:
            tiles = [(self.B.sb(st, f"{name}{k}", shape, dt), Buf(f"{name}{k}")) for k in range(n)]
            self.tmpcnt[key] = [tiles, 0]
        ent = self.tmpcnt[key]
        t = ent[0][ent[1] % len(ent[0])]
        ent[1] += 1
        return t

    def act(self, out, in_, func, reads, writes, bias=None, scale=None):
        kw = {}
        if bias is not None:
            kw["bias"] = bias
        if scale is not None:
            kw["scale"] = scale
        nc = self.nc
        self.P.op(self.P.act, lambda: nc.scalar.activation(out=out, in_=in_, func=func, **kw), reads=reads, writes=writes)

    def ts(self, out, in0, s1, op0, reads, writes, s2=None, op1=None, eng=None):
        nc = self.nc
        if op1 is None:
            self.P.op(self.P.dve, lambda: nc.vector.tensor_scalar(out=out, in0=in0, scalar1=s1, scalar2=None, op0=op0), reads=reads, writes=writes)
        else:
            self.P.op(self.P.dve, lambda: nc.vector.tensor_scalar(out=out, in0=in0, scalar1=s1, scalar2=s2, op0=op0, op1=op1), reads=reads, writes=writes)

    def stt(self, out, in0, scalar, in1, op0, op1, reads, writes):
        nc = self.nc
        self.P.op(self.P.dve, lambda: nc.vector.scalar_tensor_tensor(out=out, in0=in0, scalar=scalar, in1=in1, op0=op0, op1=op1), reads=reads, writes=writes)

    def tt(self, out, in0, in1, op, reads, writes, eng=None):
        nc = self.nc
        if eng is self.P.pool:
            self.P.op(self.P.pool, lambda: nc.gpsimd.tensor_tensor(out=out, in0=in0, in1=in1, op=op), reads=reads, writes=writes)
        else:
            self.P.op(self.P.dve, lambda: nc.vector.tensor_tensor(out=out, in0=in0, in1=in1, op=op), reads=reads, writes=writes)

    def mm(self, out, lhsT, rhs, reads, writes, start=True, stop=True):
        nc = self.nc
        self.P.op(self.P.pe, lambda: nc.tensor.matmul(out, lhsT=lhsT, rhs=rhs, start=start, stop=stop), reads=reads, writes=writes, inc=stop)

    def tr(self, out, in_, ident, reads, writes):
        nc = self.nc
        self.P.op(self.P.pe, lambda: nc.tensor.transpose(out, in_, ident), reads=reads, writes=writes)

    def load(self, out, in_, dname, wb, key):
        self.P.dma(self.P.sp, out, in_, reads=[self.B.dbuf[dname]], writes=[wb], key="L_" + wb.name)

    def store(self, out, in_, dname, rb, key):
        self.P.dma(self.P.pool, out, in_, reads=[rb], writes=[self.B.dbuf[dname]], key="S_" + rb.name, disjoint=True)

    def cums(self, st, g, gb, nch, H, tag):
        bcol = self.B.sb(st, f"bcol_{tag}", [128, 2, nch, H])
        btail = self.B.sb(st, f"btail_{tag}", [128, 2, nch, H])
        bcolb, btailb = Buf("bcol"), Buf("btail")
        n = nch * H
        for d in range(2):
            gv = g[:, d, :, :].rearrange("p c h -> p (c h)")
            for (dst, dstb, mat) in ((bcol, bcolb, "TRI"), (btail, btailb, "STR")):
                dv = dst[:, d, :, :].rearrange("p c h -> p (c h)")
                for c0 in range(0, n, 512):
                    w = min(512, n - c0)
                    p, pb = self.nps()
                    self.mm(p[:, 0:w], self.C(mat + "fb"[d]), gv[:, c0:c0 + w], [self.cstb, gb], [pb])
                    self.P.op(self.P.act, lambda: self.nc.scalar.copy(out=dv[:, c0:c0 + w], in_=p[:, 0:w]), reads=[pb], writes=[dstb])
        return bcol, bcolb, btail, btailb

    def finalize(self, st, ob, obb, tok, H, dv, gate_off, gate_func, gain, gainb, mix_off):
        W = H * dv
        of, ofb = self.tmp(st, "fin_of", [128, 2048], n=1)
        self.load(of[:, 0:W], self.B.dram["OT"][tok:tok + 128, mix_off:mix_off + W], "OT", ofb, "fin_of")
        gt, gtb = self.tmp(st, "fin_gt", [128, 2048], n=1)
        self.load(gt[:, 0:W], self.B.PT(tok, 128, gate_off, W), "PT", gtb, "fin_gt")
        self.tt(of[:, 0:W], of[:, 0:W], ob[:, 0:W], ALU.add, [ofb, obb], [ofb])
        sq, sqb = self.tmp(st, "fin_sq", [128, 2048], n=1)
        self.tt(sq[:, 0:W], of[:, 0:W], of[:, 0:W], ALU.mult, [ofb], [sqb], eng=self.P.pool)
        ss, ssb = self.tmp(st, "fin_ss", [128, 16])
        nc = self.nc
        self.P.op(self.P.dve, lambda: nc.vector.tensor_reduce(out=ss[:, 0:H], in_=sq[:, 0:W].rearrange("p (h d) -> p h d", h=H),
                                                              axis=mybir.AxisListType.X, op=ALU.add), reads=[sqb], writes=[ssb])
        self.act(ss[:, 0:H], ss[:, 0:H], AF.Ln, [ssb], [ssb], bias=EPS, scale=1.0 / dv)
        self.act(ss[:, 0:H], ss[:, 0:H], AF.Exp, [ssb], [ssb], scale=-0.5)
        self.act(gt[:, 0:W], gt[:, 0:W], gate_func, [gtb], [gtb])
        for h in range(H):
            self.stt(of[:, h * dv:(h + 1) * dv], of[:, h * dv:(h + 1) * dv], ss[:, h:h + 1], gt[:, h * dv:(h + 1) * dv], ALU.mult, ALU.mult,
                     [ofb, ssb, gtb], [ofb])
        mo, mob = self.tmp(st, "fin_mo", [128, 2048], BF16, n=1)
        self.tt(mo[:, 0:W], of[:, 0:W], gain[:, 0:W], ALU.mult, [ofb, gainb], [mob])
        self.store(self.B.dram["mixed"][tok:tok + 128, mix_off:mix_off + W], mo[:, 0:W], "mixed", mob, "fin_mo")

    def scalar_step(self, st, d, c, h, H, nkc, dv, qf, kf, qfb, kfb, v_ap, vb, gcol, colterm, etail, gtb, S, Sbf, Sb, qscale,
                    ob, obb, gdn=None, ml=None):
        nc, P = self.nc, self.P
        fb = "fb"[d]
        last = 127 if d == 0 else 0
        lim = 99
        for tk in getattr(self.B, "skip", ()):
            if tk.startswith("lim"):
                lim = float(tk[3:])
        if (lim < 99 and h > 0) or lim <= 0:
            return
        grep, grepb = self.tmp(st, "grep", [128, 128])
        self.ts(grep[:], self.C("ones"), gcol, ALU.mult, [self.cstb, gtb], [grepb])
        if lim <= 0.2:
            return
        pB, pBb = self.nps()
        self.mm(pB[:, 0:128], grep[:], self.C("TRI" + fb), [grepb, self.cstb], [pBb])
        if lim <= 0.4:
            return
        ebc, ebcb = self.tmp(st, "ebc", [128, 128])
        self.act(ebc[:], pB[:, 0:128], AF.Exp, [pBb], [ebcb])
        if lim <= 0.6:
            return
        brow, browb = self.tmp(st, "brow", [128, 128])
        P.op(P.act, lambda: nc.scalar.copy(out=brow[:], in_=pB[:, 0:128]), reads=[pBb], writes=[browb])
        m1, m1b = self.tmp(st, "m1", [128, 128])
        self.stt(m1[:], brow[:], colterm, self.C("UIN" + fb), ALU.add, ALU.add, [browb, gtb, self.cstb], [m1b])
        if lim <= 0.8:
            return
        self.act(m1[:], m1[:], AF.Exp, [m1b], [m1b])
        if lim <= 1:
            return
        qbf, qbfb = self.tmp(st, "qbf", [128, 2, 128], BF16)
        kbf, kbfb = self.tmp(st, "kbf", [128, 2, 128], BF16)
        qdec, qdecb = self.tmp(st, "qdec", [128, 2, 128], BF16)
        for kc in range(nkc):
            r = h * nkc + kc
            P.op(P.act, lambda: nc.scalar.copy(out=qbf[:, kc, :], in_=qf[:, r, :]), reads=[qfb], writes=[qbfb])
            P.op(P.pool, lambda: nc.gpsimd.tensor_copy(out=kbf[:, kc, :], in_=kf[:, r, :]), reads=[kfb], writes=[kbfb])
            self.stt(qdec[:, kc, :], qf[:, r, :], qscale, ebc[:], ALU.mult, ALU.mult, [qfb, ebcb], [qdecb])
        pA, pAb = self.nps()
        for kc in range(nkc):
            self.mm(pA[:, 0:128], kbf[:, kc, :], qbf[:, kc, :], [kbfb, qbfb], [pAb], start=(kc == 0), stop=(kc == nkc - 1))
        AT, ATb = self.tmp(st, "AT", [128, 128], BF16)
        self.tt(AT[:], m1[:], pA[:, 0:128], ALU.mult, [pAb, m1b], [ATb])
        if lim <= 2:
            return
        ktail, ktailb = self.tmp(st, "ktail", [128, 2, 128], BF16)
        pKs = []
        for kc in range(nkc):
            r = h * nkc + kc
            pK, pKb = self.nps()
            self.tr(pK[:, 0:128], kf[:, r, :], self.C("ident"), [kfb, self.cstb], [pKb])
            kt32, kt32b = self.tmp(st, "kt32", [128, 128], n=3)
            P.op(P.act, lambda: nc.scalar.copy(out=kt32[:], in_=pK[:, 0:128]), reads=[pKb], writes=[kt32b])
            self.ts(ktail[:, kc, :], kt32[:], etail, ALU.mult, [kt32b, gtb], [ktailb])
            pKs.append((kt32, kt32b))
        if lim <= 3:
            return
        if gdn is not None:
            vf, vfb, beta, beb = gdn
            pK, pKb = pKs[0]
            X, Xb = self.tmp(st, "X", [128, 256])
            self.ts(X[:, 128:256], pK[:], beb, ALU.mult, [pKb, gtb], [Xb])
            pV, pVb = self.nps()
            self.tr(pV[:, 0:128], vf[:, h, :], self.C("ident"), [vfb, self.cstb], [pVb])
            vt32, vt32b = self.tmp(st, "vt32", [128, 128])
            P.op(P.act, lambda: nc.scalar.copy(out=vt32[:], in_=pV[:, 0:128]), reads=[pVb], writes=[vt32b])
            self.ts(X[:, 0:128], vt32[:], beta, ALU.mult, [vt32b, gtb], [Xb])
            pKK, pKKb = self.nps()
            self.mm(pKK[:, 0:128], kbf[:, 0, :], kbf[:, 0, :], [kbfb], [pKKb])
            lm, lmb = self.tmp(st, "lm", [128, 128])
            self.stt(lm[:], brow[:], colterm, self.C("LST" + fb), ALU.add, ALU.subtract, [browb, gtb, self.cstb], [lmb])
            self.act(lm[:], lm[:], AF.Exp, [lmb], [lmb], scale=-1.0)
            self.ts(lm[:], lm[:], beta, ALU.mult, [lmb, gtb], [lmb])
            Pm, Pmb = self.tmp(st, "Pm", [128, 128])
            self.tt(Pm[:], lm[:], pKK[:, 0:128], ALU.mult, [pKKb, lmb], [Pmb])
            pT, pTb = self.nps()
            self.tr(pT[:, 0:128], Pm[:], self.C("ident"), [Pmb, self.cstb], [pTb])
            Qm, Qmb = self.tmp(st, "Qm", [128, 128])
            P.op(P.act, lambda: nc.scalar.copy(out=Qm[:], in_=pT[:, 0:128]), reads=[pTb], writes=[Qmb])
            if lim <= 4:
                return
            for s in range(7):
                pX, pXb = self.nps()
                self.mm(pX[:, 0:256], Qm[:], X[:], [Qmb, Xb], [pXb])
                X2, X2b = self.tmp(st, "X", [128, 256])
                self.tt(X2[:], X[:], pX[:, 0:256], ALU.subtract if s == 0 else ALU.add, [Xb, pXb], [X2b])
                X, Xb = X2, X2b
                if s == 6:
                    break
                pQ, pQb = self.nps()
                self.mm(pQ[:, 0:128], Pm[:], Qm[:], [Pmb, Qmb], [pQb])
                if s < 5:
                    pP, pPb = self.nps()
                    self.mm(pP[:, 0:128], Qm[:], Pm[:], [Pmb, Qmb], [pPb])
                    Pm2, Pm2b = self.tmp(st, "Pm", [128, 128])
                    P.op(P.act, lambda: nc.scalar.copy(out=Pm2[:], in_=pP[:, 0:128]), reads=[pPb], writes=[Pm2b])
                Qm2, Qm2b = self.tmp(st, "Qm", [128, 128])
                self.P.op(P.dve, lambda: nc.vector.tensor_copy(out=Qm2[:], in_=pQ[:, 0:128]), reads=[pQb], writes=[Qm2b])
                Qm, Qmb = Qm2, Qm2b
                if s < 5:
                    Pm, Pmb = Pm2, Pm2b
            if lim <= 5:
                return
            pW, pWb = self.nps()
            self.tr(pW[:, 0:128], X[:, 128:256], self.C("ident"), [Xb, self.cstb], [pWb])
            wT, wTb = self.tmp(st, "wT", [128, 128], BF16)
            P.op(P.act, lambda: nc.scalar.copy(out=wT[:], in_=pW[:, 0:128]), reads=[pWb], writes=[wTb])
            pWS, pWSb = self.nps()
            self.mm(pWS[:, 0:dv], wT[:], Sbf[:, h, 0, 0:dv], [wTb, Sb], [pWSb])
            vn, vnb = self.tmp(st, "vn", [128, 128], BF16)
            self.tt(vn[:], X[:, 0:128], pWS[:, 0:dv], ALU.subtract, [Xb, pWSb], [vnb])
            v_ap, vb = vn[:], vnb
        if lim <= 6:
            return
        pO, pOb = self.nps()
        self.mm(pO[:, 0:dv], AT[:], v_ap, [ATb, vb], [pOb], start=True, stop=False)
        for kc in range(nkc):
            self.mm(pO[:, 0:dv], qdec[:, kc, :], Sbf[:, h, kc, 0:dv], [qdecb, Sb], [pOb], start=False, stop=(kc == nkc - 1))
        if ml is None:
            P.op(P.act, lambda: nc.scalar.copy(out=ob[:, h * dv:(h + 1) * dv], in_=pO[:, 0:dv]), reads=[pOb], writes=[obb])
        else:
            nvec, nbf, nb_, onescol = ml
            pD, pDb = self.nps()
            self.mm(pD[:, 0:1], AT[:], onescol, [ATb, self.onesbfb], [pDb], start=True, stop=False)
            for kc in range(nkc):
                self.mm(pD[:, 0:1], qdec[:, kc, :], nbf[:, h, kc:kc + 1], [qdecb, nb_], [pDb], start=False, stop=(kc == nkc - 1))
            den, denb = self.tmp(st, "den", [128, 1])
            self.act(den[:], pD[:, 0:1], AF.Abs, [pDb], [denb])
            self.ts(den[:], den[:], 1.0, ALU.max, [denb], [denb])
            P.op(P.dve, lambda: nc.vector.reciprocal(out=den[:], in_=den[:]), reads=[denb], writes=[denb])
            P.op(P.act, lambda: nc.scalar.copy(out=ob[:, h * dv:(h + 1) * dv], in_=pO[:, 0:dv]), reads=[pOb], writes=[obb])
            self.ts(ob[:, h * dv:(h + 1) * dv], ob[:, h * dv:(h + 1) * dv], den[:, 0:1], ALU.mult, [obb, denb], [obb])
            for kc in range(nkc):
                pN, pNb = self.nps()
                self.mm(pN[:, 0:1], ktail[:, kc, :], onescol, [ktailb, self.onesbfb], [pNb])
                self.ts(nvec[:, h, kc:kc + 1], nvec[:, h, kc:kc + 1], ebc[:, last:last + 1], ALU.mult, [nb_, ebcb], [nb_])
                self.tt(nvec[:, h, kc:kc + 1], nvec[:, h, kc:kc + 1], pN[:, 0:1], ALU.add, [nb_, pNb], [nb_])
            P.op(P.act, lambda: nc.scalar.copy(out=nbf[:, h, :], in_=nvec[:, h, :]), reads=[nb_], writes=[nb_])
        for kc in range(nkc):
            pS, pSb = self.nps()
            self.mm(pS[:, 0:dv], ktail[:, kc, :], v_ap, [ktailb, vb], [pSb])
            self.ts(S[:, h, kc, 0:dv], S[:, h, kc, 0:dv], ebc[:, last:last + 1], ALU.mult, [Sb, ebcb], [Sb])
            self.tt(S[:, h, kc, 0:dv], S[:, h, kc, 0:dv], pS[:, 0:dv], ALU.add, [Sb, pSb], [Sb])
            P.op(P.act, lambda: nc.scalar.copy(out=Sbf[:, h, kc, 0:dv], in_=S[:, h, kc, 0:dv]), reads=[Sb], writes=[Sb])

    def seq_chunks(self, L, d):
        n = L // 128
        return list(range(n)) if d == 0 else list(range(n - 1, -1, -1))

    def load_qk(self, st, rows0, nrows, tok, name):
        t, tb = self.tmp(st, name, [128, 16, 128])
        nr = nrows // 128
        self.load(t[:, 0:nr, :], self.B.PF(rows0, nrows, tok, 128).rearrange("(r p) t -> p r t", p=128), "PF", tb, name)
        return t, tb

    def mlstm_ret(self, t0, L, which):
        B, nc, P = self.B, self.nc, self.P
        nch = L // 128
        Fg, Tg = col_plan(False)
        if which == "C":
            H, nkc, dv = 4, 2, 512
            qoff, koff = group_off(Fg, "qc"), group_off(Fg, "kc")
            voff, goff, mix_off = group_off(Tg, "vc"), group_off(Tg, "oc"), 0
            gfunc = AF.Sigmoid
        else:
            H, nkc, dv = 8, 2, 256
            qoff, koff = group_off(Fg, "qd"), group_off(Fg, "kd")
            voff, goff, mix_off = group_off(Tg, "vd"), group_off(Tg, "gd"), 2048
            gfunc = AF.Silu
        gate_off = group_off(Tg, "gate")
        qscale = 1.0 / 16.0
        with contextlib.ExitStack() as st:
            self.tmpcnt = {}
            sb = lambda n, s, d=F32: B.sb(st, n, s, d)
            gt = sb("g_all", [128, 2, nch, H]); gtb = Buf("gates")
            li = sb("li_all", [128, 2, nch, H])
            prm = sb("prm", [128, 64])
            self.load(prm[:], B.dram[f"prm_odd{self.i}"][:, :], f"prm_odd{self.i}", gtb, "m_prm")
            gain = sb("gain", [128, 2048]); gainb = Buf("gain")
            self.load(gain[:], B.dram[f"gain_{which}{self.i}"][:, :], f"gain_{which}{self.i}", gainb, "m_gain")
            if which == "C":
                raw = sb("graw", [128, nch, 16])
                self.load(raw[:], B.PT(t0, L, gate_off, 16).rearrange("(c p) g -> p c g", p=128), "PT", gtb, "m_graw")
                for d in range(2):
                    igb = prm[:, d * 4:(d + 1) * 4].unsqueeze(1).to_broadcast([128, nch, 4])
                    fgb = prm[:, 8 + d * 4:8 + (d + 1) * 4].unsqueeze(1).to_broadcast([128, nch, 4])
                    self.tt(li[:, d, :, :], raw[:, :, d * 4:(d + 1) * 4], igb, ALU.add, [gtb], [gtb])
                    self.tt(gt[:, d, :, :], raw[:, :, 8 + d * 4:8 + (d + 1) * 4], fgb, ALU.add, [gtb], [gtb])
                self.act(gt[:], gt[:], AF.Sigmoid, [gtb], [gtb])
                self.act(gt[:], gt[:], AF.Ln, [gtb], [gtb])
            else:
                for d in range(2):
                    lg = prm[:, 16 + d * 8:16 + (d + 1) * 8].unsqueeze(1).to_broadcast([128, nch, 8])
                    P.op(P.dve, lambda: nc.vector.tensor_copy(out=gt[:, d, :, :], in_=lg), reads=[gtb], writes=[gtb])
                P.op(P.pool, lambda: nc.gpsimd.memset(li[:], 0.0), writes=[gtb])
            bcol, bcolb, btail, btailb = self.cums(st, gt, gtb, nch, H, which)
            self.tt(bcol[:], li[:], bcol[:], ALU.subtract, [gtb, bcolb], [bcolb])
            self.ts(bcol[:], bcol[:], math.log(qscale), ALU.add, [bcolb], [bcolb])
            self.tt(btail[:], btail[:], li[:], ALU.add, [gtb, btailb], [btailb])
            self.act(btail[:], btail[:], AF.Exp, [btailb], [btailb])
            P.op(P.dve, lambda: nc.vector.tensor_copy(out=li[:, 0, 0, 0:1], in_=li[:, 0, 0, 0:1]), reads=[bcolb, btailb, gtb], writes=[gtb])
            S = sb("S", [128, H, nkc, dv]); Sbf = sb("Sbf", [128, H, nkc, dv], BF16); Sb = Buf("S")
            nvec = sb("nvec", [128, H, nkc]); nbf = sb("nbf", [128, H, nkc], BF16); nb_ = Buf("nvec")
            for d in range(2):
                P.op(P.pool, lambda: nc.gpsimd.memset(S[:], 0.0), writes=[Sb])
                P.op(P.pool, lambda: nc.gpsimd.memset(Sbf[:], 0.0), writes=[Sb])
                P.op(P.pool, lambda: nc.gpsimd.memset(nvec[:], 0.0), writes=[nb_])
                P.op(P.pool, lambda: nc.gpsimd.memset(nbf[:], 0.0), writes=[nb_])
                for c in self.seq_chunks(L, d):
                    tok = t0 + c * 128
                    qf, qfb = self.load_qk(st, qoff, H * nkc * 128, tok, "qf")
                    kf, kfb = self.load_qk(st, koff, H * nkc * 128, tok, "kf")
                    vf, vfb = self.tmp(st, "vf", [128, 2048])
                    self.load(vf[:], B.PT(tok, 128, voff, 2048), "PT", vfb, "vf")
                    vbf, vbfb = self.tmp(st, "vbf", [128, 2048], BF16)
                    P.op(P.pool, lambda: nc.gpsimd.tensor_copy(out=vbf[:], in_=vf[:]), reads=[vfb], writes=[vbfb])
                    ob, obb = self.tmp(st, "ob", [128, 2048])
                    for h in range(H):
                        ml = (nvec, nbf, nb_, self.onesbf[:, 0:1]) if which == "C" else None
                        self.scalar_step(st, d, c, h, H, nkc, dv, qf, kf, qfb, kfb, vbf[:, h * dv:(h + 1) * dv], vbfb,
                                         gt[:, d, c, h:h + 1], bcol[:, d, c, h:h + 1], btail[:, d, c, h:h + 1], gtb, S, Sbf, Sb, qscale,
                                         ob, obb, ml=ml)
                    if d == 0:
                        self.store(B.dram["OT"][tok:tok + 128, mix_off:mix_off + 2048], ob[:], "OT", obb, "ob_st")
                    else:
                        self.finalize(st, ob, obb, tok, H, dv, goff, gfunc, gain, gainb, mix_off)

    def rotary(self):
        B, nc, P = self.B, self.nc, self.P
        Fg, _ = col_plan(False)
        with contextlib.ExitStack() as st:
            self.tmpcnt = {}
            LB = 512
            t0 = 0
            for L in B.seqs:
                for l0 in range(0, L, LB):
                    cs, csb = self.tmp(st, "rcos", [128, LB])
                    sn, snb = self.tmp(st, "rsin", [128, LB])
                    self.load(cs[:], B.dram["rcos"][:, l0:l0 + LB], "rcos", csb, "rcos")
                    self.load(sn[:], B.dram["rsin"][:, l0:l0 + LB], "rsin", snb, "rsin")
                    for name in ("qd", "kd"):
                        off = group_off(Fg, name)
                        for h in range(8):
                            r1 = off + h * 256
                            x1, x1b = self.tmp(st, "rx1", [128, LB])
                            x2, x2b = self.tmp(st, "rx2", [128, LB])
                            self.load(x1[:], B.PF(r1, 128, t0 + l0, LB), "PF", x1b, "rx1")
                            self.load(x2[:], B.PF(r1 + 128, 128, t0 + l0, LB), "PF", x2b, "rx2")
                            a, ab = self.tmp(st, "ra", [128, LB])
                            b, bb = self.tmp(st, "rb", [128, LB])
                            y1, y1b = self.tmp(st, "ry1", [128, LB])
                            y2, y2b = self.tmp(st, "ry2", [128, LB])
                            self.tt(a[:], x1[:], cs[:], ALU.mult, [x1b, csb], [ab])
                            self.tt(b[:], x2[:], sn[:], ALU.mult, [x2b, snb], [bb], eng=P.pool)
                            self.tt(y1[:], a[:], b[:], ALU.subtract, [ab, bb], [y1b])
                            self.tt(a[:], x1[:], sn[:], ALU.mult, [x1b, snb], [ab])
                            self.tt(b[:], x2[:], cs[:], ALU.mult, [x2b, csb], [bb], eng=P.pool)
                            self.tt(y2[:], a[:], b[:], ALU.add, [ab, bb], [y2b])
                            self.store(B.PF(r1, 128, t0 + l0, LB), y1[:], "PF", y1b, "ry1")
                            self.store(B.PF(r1 + 128, 128, t0 + l0, LB), y2[:], "PF", y2b, "ry2")
                t0 += L
        P.barrier()

    def gdn_prep(self, t0, L):
        B, nc, P = self.B, self.nc, self.P
        with contextlib.ExitStack() as st:
            self.tmpcnt = {}
            cw = B.sb(st, "cw", [128, 48 * 5]); cwb = Buf("cw")
            self.load(cw[:], B.dram[f"convw{self.i}"][:, :], f"convw{self.i}", cwb, "g_cw")
            for rc in range(48):
                xp, xpb = self.tmp(st, "xp", [128, L + 4])
                P.op(P.pool, lambda: nc.gpsimd.memset(xp[:, 0:2], 0.0), writes=[xpb])
                P.op(P.pool, lambda: nc.gpsimd.memset(xp[:, L + 2:L + 4], 0.0), writes=[xpb])
                self.load(xp[:, 2:L + 2], B.PF(rc * 128, 128, t0, L), "PF", xpb, "g_xp")
                acc, accb = self.tmp(st, "acc", [128, L])
                self.ts(acc[:], xp[:, 0:L], cw[:, rc * 5:rc * 5 + 1], ALU.mult, [xpb, cwb], [accb])
                for tau in range(1, 5):
                    self.stt(acc[:], xp[:, tau:tau + L], cw[:, rc * 5 + tau:rc * 5 + tau + 1], acc[:], ALU.mult, ALU.add, [xpb, cwb, accb], [accb])
                self.act(acc[:], acc[:], AF.Silu, [accb], [accb])
                if rc < 32:
                    sq, sqb = self.tmp(st, "sq", [128, 512])
                    for c0 in range(0, L, 512):
                        self.tt(sq[:], acc[:, c0:c0 + 512], acc[:, c0:c0 + 512], ALU.mult, [accb], [sqb], eng=P.pool)
                        p, pb = self.nps()
                        self.mm(p[:], self.C("ones"), sq[:], [self.cstb, sqb], [pb])
                        self.act(sq[:], p[:], AF.Ln, [pb], [sqb], bias=EPS)
                        self.act(sq[:], sq[:], AF.Exp, [sqb], [sqb], scale=-0.5)
                        if rc < 16:
                            self.stt(acc[:, c0:c0 + 512], acc[:, c0:c0 + 512], 128.0 ** -0.5, sq[:], ALU.mult, ALU.mult, [accb, sqb], [accb])
                        else:
                            self.tt(acc[:, c0:c0 + 512], acc[:, c0:c0 + 512], sq[:], ALU.mult, [accb, sqb], [accb])
                self.store(B.PF(rc * 128, 128, t0, L), acc[:], "PF", accb, "g_acc")
        P.barrier()

    def gdn(self, t0, L):
        B, nc, P = self.B, self.nc, self.P
        skip = getattr(B, "skip", set())
        if "Anoprep" not in skip:
            self.gdn_prep(t0, L)
        if "Astop1" in skip:
            return
        nch = L // 128
        H, dv = 16, 128
        Fg, Tg = col_plan(True)
        gate_off = group_off(Tg, "gate")
        with contextlib.ExitStack() as st:
            self.tmpcnt = {}
            sb = lambda n, s, d=F32: B.sb(st, n, s, d)
            gtb = Buf("gates")
            raw = sb("graw", [128, nch, 64])
            self.load(raw[:], B.PT(t0, L, gate_off, 64).rearrange("(c p) g -> p c g", p=128), "PT", gtb, "m_graw")
            prm = sb("prm", [128, 64])
            self.load(prm[:], B.dram[f"prm_even{self.i}"][:, :], f"prm_even{self.i}", gtb, "m_prm")
            self.act(prm[:, 0:32], prm[:, 0:32], AF.Exp, [gtb], [gtb])
            gain = sb("gain", [128, 2048]); gainb = Buf("gain")
            self.load(gain[:], B.dram[f"gain_A{self.i}"][:, :], f"gain_A{self.i}", gainb, "m_gain")
            gt = sb("g_all", [128, 2, nch, H])
            beta = sb("beta", [128, 2, nch, H])
            beb = sb("beb", [128, 2, nch, H])
            for d in range(2):
                dtb = prm[:, 32 + d * 16:32 + (d + 1) * 16].unsqueeze(1).to_broadcast([128, nch, 16])
                ea = prm[:, d * 16:(d + 1) * 16].unsqueeze(1).to_broadcast([128, nch, 16])
                self.tt(gt[:, d, :, :], raw[:, :, d * 16:(d + 1) * 16], dtb, ALU.add, [gtb], [gtb])
                self.act(gt[:, d, :, :], gt[:, d, :, :], AF.Exp, [gtb], [gtb])
                self.act(gt[:, d, :, :], gt[:, d, :, :], AF.Ln, [gtb], [gtb], bias=1.0)
                self.tt(gt[:, d, :, :], gt[:, d, :, :], ea, ALU.mult, [gtb], [gtb])
                self.act(beta[:, d, :, :], raw[:, :, 32 + d * 16:32 + (d + 1) * 16], AF.Sigmoid, [gtb], [gtb])
            self.ts(gt[:], gt[:], -1.0, ALU.mult, [gtb], [gtb])
            bcol, bcolb, btail, btailb = self.cums(st, gt, gtb, nch, H, "A")
            self.act(beb[:], bcol[:], AF.Exp, [bcolb], [gtb])
            self.tt(beb[:], beb[:], beta[:], ALU.mult, [gtb], [gtb])
            self.ts(bcol[:], bcol[:], -1.0, ALU.mult, [bcolb], [bcolb])
            self.act(btail[:], btail[:], AF.Exp, [btailb], [btailb])
            P.op(P.dve, lambda: nc.vector.tensor_copy(out=beb[:, 0, 0, 0:1], in_=beb[:, 0, 0, 0:1]), reads=[bcolb, btailb, gtb], writes=[gtb])
            S = sb("S", [128, H, 1, dv]); Sbf = sb("Sbf", [128, H, 1, dv], BF16); Sb = Buf("S")
            if "Astop2" in skip:
                return
            for d in range(2):
                P.op(P.pool, lambda: nc.gpsimd.memset(S[:], 0.0), writes=[Sb])
                P.op(P.pool, lambda: nc.gpsimd.memset(Sbf[:], 0.0), writes=[Sb])
                for c in self.seq_chunks(L, d):
                    if "Astop3" in skip and not (d == 0 and c == 0):
                        continue
                    tok = t0 + c * 128
                    qf, qfb = self.load_qk(st, 0, 2048, tok, "qf")
                    kf, kfb = self.load_qk(st, 2048, 2048, tok, "kf")
                    vf, vfb = self.load_qk(st, 4096, 2048, tok, "vfF")
                    ob, obb = self.tmp(st, "ob", [128, 2048])
                    for h in range(H):
                        self.scalar_step(st, d, c, h, H, 1, dv, qf, kf, qfb, kfb, None, None,
                                         gt[:, d, c, h:h + 1], bcol[:, d, c, h:h + 1], btail[:, d, c, h:h + 1], gtb, S, Sbf, Sb, 1.0,
                                         ob, obb, gdn=(vf, vfb, beta[:, d, c, h:h + 1], beb[:, d, c, h:h + 1]))
                    if d == 0:
                        self.store(B.dram["OT"][tok:tok + 128, 0:2048], ob[:], "OT", obb, "ob_st")
                    else:
                        self.finalize(st, ob, obb, tok, H, dv, group_off(Tg, "ga"), AF.Silu, gain, gainb, 0)

    def hgrn(self, t0, L):
        B, nc, P = self.B, self.nc, self.P
        H, dv = 16, 128
        Fg, Tg = col_plan(True)
        qoff = group_off(Fg, "qb")
        voff, goff = group_off(Tg, "ib"), group_off(Tg, "gb")
        with contextlib.ExitStack() as st:
            self.tmpcnt = {}
            sb = lambda n, s, d=F32: B.sb(st, n, s, d)
            lbt = sb("lbt", [128, 64]); lbb = Buf("lb")
            self.load(lbt[:], B.dram["lb_cols"][:, :], "lb_cols", lbb, "m_lb")
            lbc = sb("lbc", [128, 32]); oml = sb("oml", [128, 32])
            if self.i == 0:
                P.op(P.pool, lambda: nc.gpsimd.memset(lbc[:], 0.0), writes=[lbb])
            else:
                self.act(lbt[:], lbt[:], AF.Exp, [lbb], [lbb])
                self.tt(lbc[:], lbt[:, 0:32], lbt[:, 32:64], ALU.add, [lbb], [lbb])
                P.op(P.dve, lambda: nc.vector.reciprocal(out=lbc[:], in_=lbc[:]), reads=[lbb], writes=[lbb])
                self.tt(lbc[:], lbc[:], lbt[:, 32:64], ALU.mult, [lbb], [lbb])
            self.ts(oml[:], lbc[:], -1.0, ALU.mult, [lbb], [lbb], s2=1.0, op1=ALU.add)
            gain = sb("gain", [128, 2048]); gainb = Buf("gain")
            self.load(gain[:], B.dram[f"gain_B{self.i}"][:, :], f"gain_B{self.i}", gainb, "m_gain")
            S = sb("S", [128, H, dv]); Sbf = sb("Sbf", [128, H, dv], BF16); Sb = Buf("S")
            ATd = [sb(f"ATd{d}", [128, 128], BF16) for d in range(2)]
            ATb = [Buf("ATd0"), Buf("ATd1")]
            cumx = sb("cumx", [128, 129]); cumxb = Buf("cumx")
            P.op(P.pool, lambda: nc.gpsimd.memset(cumx[:], 0.0), writes=[cumxb])
            ncol = sb("ncol", [128, 2])
            kAt = [[sb(f"kA{d}_{k}", [128, 128], BF16) for k in range(2)] for d in range(2)]
            kAb = [[Buf(f"kA{d}_{k}") for k in range(2)] for d in range(2)]
            kacnt = [0, 0]
            for d in range(2):
                for k in range(2):
                    P.op(P.pool, lambda: nc.gpsimd.memset(kAt[d][k][:], 0.0), writes=[kAb[d][k]])
            for d in range(2):
                P.op(P.pool, lambda: nc.gpsimd.memset(ATd[d][:], 0.0), writes=[ATb[d]])
                P.op(P.pool, lambda: nc.gpsimd.memset(S[:], 0.0), writes=[Sb])
                P.op(P.pool, lambda: nc.gpsimd.memset(Sbf[:], 0.0), writes=[Sb])
                zoff = group_off(Fg, "zf" if d == 0 else "zb")
                for c in self.seq_chunks(L, d):
                    tok = t0 + c * 128
                    qf, qfb = self.load_qk(st, qoff, 2048, tok, "qf")
                    zf, zfb = self.load_qk(st, zoff, 2048, tok, "kf")
                    vf, vfb = self.tmp(st, "vf", [128, 2048])
                    self.load(vf[:], B.PT(tok, 128, voff, 2048), "PT", vfb, "vf")
                    vbf, vbfb = self.tmp(st, "vbf", [128, 2048], BF16)
                    P.op(P.pool, lambda: nc.gpsimd.tensor_copy(out=vbf[:], in_=vf[:]), reads=[vfb], writes=[vbfb])
                    ob, obb = self.tmp(st, "ob", [128, 2048])
                    for h in range(H):
                        col = d * 16 + h
                        f, fb_ = self.tmp(st, "hf", [128, 128])
                        self.act(f[:], zf[:, h, :], AF.Sigmoid, [zfb], [fb_])
                        self.ts(f[:], f[:], oml[:, col:col + 1], ALU.mult, [fb_, lbb], [fb_], s2=lbc[:, col:col + 1], op1=ALU.add)
                        k, kb = self.tmp(st, "hk", [128, 128])
                        self.ts(k[:], f[:], -1.0, ALU.mult, [fb_], [kb], s2=1.0, op1=ALU.add)
                        self.act(f[:], f[:], AF.Ln, [fb_], [fb_])
                        P.op(P.dve, lambda: nc.vector.tensor_tensor_scan(out=cumx[:, 1:129], data0=self.C("ones"), data1=f[:], initial=0.0,
                                                                         op0=ALU.mult, op1=ALU.add), reads=[fb_, self.cstb], writes=[cumxb])
                        self.ts(ncol[:, 0:1], cumx[:, 64:65], -1.0, ALU.mult, [cumxb], [cumxb])
                        self.ts(ncol[:, 1:2], cumx[:, 128:129], -1.0, ALU.mult, [cumxb], [cumxb])
                        RM, nRM, TT_, nT = cumx[:, 64:65], ncol[:, 0:1], cumx[:, 128:129], ncol[:, 1:2]
                        X, Y = slice(0, 64), slice(64, 128)
                        if d == 0:
                            cs = cumx[:, 1:129]
                            specs = [("kA", "k", X, -1.0, None), ("qA", "q", X, 1.0, None),
                                     ("kB", "k", slice(0, 128), -1.0, RM), ("qB", "q", Y, 1.0, nRM),
                                     ("qdec", "q", slice(0, 128), 1.0, None), ("ktT", "k", slice(0, 128), -1.0, TT_)]
                        else:
                            cs = cumx[:, 0:128]
                            specs = [("kA", "k", Y, 1.0, nT), ("qA", "q", Y, -1.0, TT_),
                                     ("kB", "k", slice(0, 128), 1.0, nRM), ("qB", "q", X, -1.0, RM),
                                     ("qdec", "q", slice(0, 128), -1.0, TT_), ("ktT", "k", slice(0, 128), 1.0, None)]
                        res = {}
                        for si, (nm, srcn, cols, sc, bi) in enumerate(specs):
                            w = cols.stop - cols.start
                            e, eb_ = self.tmp(st, f"he{si}", [128, 128])
                            self.act(e[:, 0:w], cs[:, cols], AF.Exp, [cumxb], [eb_], bias=bi, scale=sc)
                            src, srcb = (qf[:, h, cols], qfb) if srcn == "q" else (k[:, cols], kb)
                            if nm == "kA":
                                ka_i = kacnt[d] % 2
                                kacnt[d] += 1
                                o_, ob_ = kAt[d][ka_i], kAb[d][ka_i]
                                self.tt(o_[:, cols], src, e[:, 0:w], ALU.mult, [srcb, eb_], [ob_])
                            else:
                                o_, ob_ = self.tmp(st, f"ho{si}", [128, 128], BF16)
                                self.tt(o_[:, 0:w], src, e[:, 0:w], ALU.mult, [srcb, eb_], [ob_], eng=(P.pool if si % 2 else None))
                            res[nm] = (o_, ob_)
                        (kA, kAb_), (qA, qAb), (kB, kBb), (qB, qBb) = res["kA"], res["qA"], res["kB"], res["qB"]
                        (qdec, qdecb), (ktT, ktTb) = res["qdec"], res["ktT"]
                        dec, decb = self.tmp(st, "hdec", [128, 1])
                        self.act(dec[:], cumx[:, 128:129], AF.Exp, [cumxb], [decb])
                        pA, pAb = self.nps()
                        ca, cb_ = (X, Y) if d == 0 else (Y, X)
                        self.mm(pA[:, ca], kA[:], qA[:, 0:64], [kAb_, qAb], [pAb])
                        self.mm(pA[:, cb_], kB[:], qB[:, 0:64], [kBb, qBb], [pAb])
                        self.tt(ATd[d][:], self.C("TRI" + "fb"[d]), pA[:, 0:128], ALU.mult, [pAb, self.cstb], [ATb[d]])
                        self.tr(self.ptb[:, 0:128], ktT[:], self.identbf[:], [ktTb, self.identbfb], [self.ptbb])
                        ktail, ktailb = self.tmp(st, "hkt", [128, 128], BF16)
                        P.op(P.act, lambda: nc.scalar.copy(out=ktail[:], in_=self.ptb[:, 0:128]), reads=[self.ptbb], writes=[ktailb])
                        v_ap = vbf[:, h * dv:(h + 1) * dv]
                        pO, pOb = self.nps()
                        self.mm(pO[:, 0:dv], ATd[d][:], v_ap, [ATb[d], vbfb], [pOb], start=True, stop=False)
                        self.mm(pO[:, 0:dv], qdec[:], Sbf[:, h, :], [qdecb, Sb], [pOb], start=False, stop=True)
                        P.op(P.act, lambda: nc.scalar.copy(out=ob[:, h * dv:(h + 1) * dv], in_=pO[:, 0:dv]), reads=[pOb], writes=[obb])
                        pS, pSb = self.nps()
                        self.mm(pS[:, 0:dv], ktail[:], v_ap, [ktailb, vbfb], [pSb])
                        self.ts(S[:, h, :], S[:, h, :], dec[:, 0:1], ALU.mult, [Sb, decb], [Sb])
                        self.tt(S[:, h, :], S[:, h, :], pS[:, 0:dv], ALU.add, [Sb, pSb], [Sb])
                        P.op(P.act, lambda: nc.scalar.copy(out=Sbf[:, h, :], in_=S[:, h, :]), reads=[Sb], writes=[Sb])
                    if d == 0:
                        self.store(B.dram["OT"][tok:tok + 128, 2048:4096], ob[:], "OT", obb, "ob_st")
                    else:
                        self.finalize(st, ob, obb, tok, H, dv, goff, AF.Silu, gain, gainb, 2048)


def build_program(seqs, depth, debug_io=False, skip=()):
    B = Builder(seqs, depth, debug_io=debug_io)
    B.skip = set(skip)
    B.declare()
    ne, no = (depth + 1) // 2, depth // 2
    B.din("cst", [128, len(CST_NAMES) * 128])
    B.din("msk", [128, 256], I32)
    B.din("lb_cols", [128, 64])
    B.din("rcos", [128, max(seqs)])
    B.din("rsin", [128, max(seqs)])
    for i in range(ne):
        B.din(f"convw{i}", [128, 240])
        B.din(f"prm_even{i}", [128, 64])
        B.din(f"gain_A{i}", [128, 2048])
        B.din(f"gain_B{i}", [128, 2048])
    for i in range(no):
        B.din(f"prm_odd{i}", [128, 64])
        B.din(f"gain_C{i}", [128, 2048])
        B.din(f"gain_D{i}", [128, 2048])
    B.prologue()
    for layer in range(depth + 1):
        B.dense_phase(layer - 1 if layer > 0 else None, layer if layer < depth else None)
        if layer < depth:
            m = Mixer(B, layer)
            m.run()
    B.P.barrier()
    return B


def bc(row, n=128):
    return np.ascontiguousarray(np.broadcast_to(np.asarray(row, np.float32).reshape(1, -1), (n, np.asarray(row).size)))


def host_inputs(depth, seqs, inputs):
    ne, no = (depth + 1) // 2, depth // 2
    f = lambda k: np.asarray(inputs[k], np.float32)
    m = {}
    for k in ("w_in_even", "w_out_even", "w_in_odd", "w_out_odd", "w_up", "w_down"):
        if k.endswith("odd") and no == 0:
            continue
        m[k] = f(k)[: (ne if "even" in k else no if "odd" in k else depth)]
    gains = np.concatenate([f("norm_mix")[:depth], f("norm_mlp")[:depth], f("norm_final")[None]], 0)
    m["gains"] = np.ascontiguousarray(gains.reshape(2 * depth + 1, 32, 128).transpose(2, 0, 1).reshape(128, -1))
    m["ident"] = np.eye(128, dtype=np.float32)
    _, cst, msk = host_consts()
    m["cst"], m["msk"] = cst, msk
    lb = f("lb_b")
    cols = []
    for layer in range(2):
        for d in range(2):
            cols.append(lb[d, min(layer, lb.shape[1] - 1)].reshape(16, 128).T)
    m["lb_cols"] = np.ascontiguousarray(np.concatenate(cols, 1))
    Lm = max(seqs)
    half = 128
    inv = (10000.0 ** (-np.arange(half, dtype=np.float32) / half)).astype(np.float32)
    ang = (np.arange(Lm, dtype=np.float32)[None, :] * inv[:, None]).astype(np.float32)
    m["rcos"] = np.cos(ang).astype(np.float32)
    m["rsin"] = np.sin(ang).astype(np.float32)
    for i in range(ne):
        cw = f("conv_a")[i]
        m[f"convw{i}"] = np.ascontiguousarray(cw.reshape(5, 48, 128).transpose(2, 1, 0).reshape(128, 240))
        m[f"prm_even{i}"] = bc(np.concatenate([f("a_log_a")[i].reshape(-1), f("dt_bias_a")[i].reshape(-1)]))
        m[f"gain_A{i}"] = bc(np.tile(f("norm_a")[i], 16))
        m[f"gain_B{i}"] = bc(np.tile(f("norm_b")[i], 16))
    lg_f = np.log1p(-np.exp2(-5.0 - np.arange(8, dtype=np.float32))).astype(np.float32)
    for i in range(no):
        row = np.zeros(64, np.float32)
        row[0:8] = f("ig_bias_c")[i].reshape(-1)
        row[8:16] = f("fg_bias_c")[i].reshape(-1)
        row[16:24] = lg_f
        row[24:32] = lg_f[::-1]
        m[f"prm_odd{i}"] = bc(row)
        m[f"gain_C{i}"] = bc(np.tile(f("norm_c")[i], 4))
        m[f"gain_D{i}"] = bc(np.ones(2048, np.float32))
    return m


DEPTH = 4


def kernel(**inputs):
    xp = np.asarray(inputs["x_prompt"], np.float32)
    xs = np.asarray(inputs["x_sample"], np.float32)
    n = 8
    Lp, Ls = xp.shape[1], xs.shape[1]
    seqs = [Lp, Ls]
    B = build_program(seqs, DEPTH)
    shared = host_inputs(DEPTH, seqs, inputs)
    in_maps = []
    for c in range(n):
        xT = np.ascontiguousarray(np.concatenate([xp[c], xs[c // 4]], axis=0).T)
        mp = dict(shared)
        mp["xT"] = xT
        in_maps.append(mp)
    res = run_bass_kernel_spmd(B.nc, in_maps, core_ids=list(range(n)))
    yp = np.stack([res.results[c]["yT"][:, :Lp].T for c in range(n)], 0)
    ys = np.stack([res.results[c * 4]["yT"][:, Lp:].T for c in range(2)], 0)
    return (np.ascontiguousarray(yp, dtype=np.float32), np.ascontiguousarray(ys, dtype=np.float32))
```
